# Optimizing a Trainium2 kernel written in Bass

```python
import jax, jax.numpy as jnp
from jax import lax
import numpy as np

D_MODEL = 1024
BATCH = 1
SEQ = 16384
DEPTH = 4
DEC_BATCH = 8
DEC_SEQ = 16
PAST_LEN = 1024

CHUNK = 64
D_A = 256
HEADS_A = 4
D_B = 512
HEADS_B = 8
HEAD_DIM_B = D_B // HEADS_B
D_C = 256
HEADS_C = 4
D_MIX = D_A + D_B + D_C
D_IN = 3 * D_A + 2 * D_B + 2 * D_C
CONV_A = 3
CONV_B = 4
CONV_C = 31
RG_C = 8.0
D_FF = 2816
N_SUB = 3
EPS = 1e-6

kernel_name = 'hybrid_streaming_encoder_step'


def rmsnorm(x, g):
    xf = x.astype(jnp.float32)
    y = xf * lax.rsqrt(jnp.mean(xf * xf, axis=-1, keepdims=True) + EPS)
    return (y * g.astype(jnp.float32)).astype(x.dtype)


def layernorm(x, g, b):
    xf = x.astype(jnp.float32)
    mu = jnp.mean(xf, axis=-1, keepdims=True)
    var = jnp.mean(jnp.square(xf - mu), axis=-1, keepdims=True)
    y = (xf - mu) * lax.rsqrt(var + EPS)
    return (y * g.astype(jnp.float32) + b.astype(jnp.float32)).astype(x.dtype)


def causal_dwconv(u, buf, w):
    k = w.shape[0]
    xp = jnp.concatenate([buf.astype(u.dtype), u], axis=1)
    y = lax.conv_general_dilated(xp, w[:, None, :].astype(u.dtype), window_strides=(1,),
                                 padding='VALID', dimension_numbers=('NWC', 'WIO', 'NWC'),
                                 feature_group_count=u.shape[-1])
    return y, xp[:, xp.shape[1] - (k - 1):]


def swiglu(h, w1, w3, w2):
    return (jax.nn.silu(h @ w1) * (h @ w3)) @ w2


def _lin_combine(left, right):
    a_l, b_l = left
    a_r, b_r = right
    return a_l * a_r, a_r * b_l + b_r


def rglru(xb, h0, w_r, b_r, w_i, b_i, lam):
    bsz, t, _ = xb.shape
    xh = xb.reshape(bsz, t, HEADS_B, HEAD_DIM_B)
    r = jax.nn.sigmoid((jnp.einsum('bthi,hij->bthj', xh, w_r).reshape(bsz, t, D_B) + b_r).astype(jnp.float32))
    i = jax.nn.sigmoid((jnp.einsum('bthi,hij->bthj', xh, w_i).reshape(bsz, t, D_B) + b_i).astype(jnp.float32))
    log_a = RG_C * r * jax.nn.log_sigmoid(lam.astype(jnp.float32))
    a = jnp.exp(log_a)
    bterm = jnp.sqrt(-jnp.expm1(2.0 * log_a)) * i * xb.astype(jnp.float32)
    bterm = bterm.at[:, 0].add(a[:, 0] * h0.astype(jnp.float32))
    _, h = lax.associative_scan(_lin_combine, (a, bterm), axis=1)
    return h.astype(xb.dtype), h[:, -1].astype(h0.dtype)


def token_mixer(h, st_a, st_b, st_h, st_c, w_in, w_out, w_conv_a, w_conv_b, b_conv_b,
                w_gate_r, b_gate_r, w_gate_i, b_gate_i, rg_lambda, w_conv_c, b_conv_c,
                ln_c_g, ln_c_b, grp_g):
    u = h @ w_in
    splits = [D_A, 2 * D_A, 3 * D_A, 3 * D_A + D_B, 3 * D_A + 2 * D_B, 3 * D_A + 2 * D_B + D_C]
    a_b, a_c, a_x, b_x, b_g, c_v, c_g = jnp.split(u, splits, axis=-1)
    ya, new_a = causal_dwconv(a_c * a_x, st_a, w_conv_a)
    ya = a_b * ya
    xb, new_b = causal_dwconv(b_x, st_b, w_conv_b)
    yb, new_h = rglru(xb + b_conv_b, st_h, w_gate_r, b_gate_r, w_gate_i, b_gate_i, rg_lambda)
    yb = yb * jax.nn.gelu(b_g)
    v = c_v * jax.nn.sigmoid(c_g)
    yc, new_c = causal_dwconv(v, st_c, w_conv_c)
    yc = jax.nn.silu(layernorm(yc + b_conv_c, ln_c_g, ln_c_b))
    y = jnp.concatenate([rmsnorm(ya, grp_g[:D_A]),
                         rmsnorm(yb, grp_g[D_A:D_A + D_B]),
                         rmsnorm(yc, grp_g[D_A + D_B:])], axis=-1)
    return y @ w_out, new_a, new_b, new_h, new_c


def setup_inputs(seed: int = 0) -> dict:
    key = jax.random.key(seed)
    ks = jax.random.split(key, 40)
    f32 = jnp.float32

    def nrm(k, shape, scale):
        return jax.random.normal(k, shape, f32) * scale

    s = D_MODEL ** -0.5
    u_lam = jax.random.uniform(ks[21], (DEPTH, D_B), f32, 0.9, 0.999)
    return {
        'x_prompt': nrm(ks[0], (BATCH, SEQ, D_MODEL), 1.0),
        'x_sample': nrm(ks[1], (DEC_BATCH, DEC_SEQ, D_MODEL), 1.0),
        'state_conv_a': nrm(ks[2], (DEPTH, DEC_BATCH, CONV_A - 1, D_A), 1.0),
        'state_conv_b': nrm(ks[3], (DEPTH, DEC_BATCH, CONV_B - 1, D_B), 1.0),
        'state_rglru': nrm(ks[4], (DEPTH, DEC_BATCH, D_B), 0.5),
        'state_conv_c': nrm(ks[5], (DEPTH, DEC_BATCH, CONV_C - 1, D_C), 1.0),
        'c_prompt': nrm(ks[6], (BATCH, D_MODEL), 1.0),
        'c_sample': nrm(ks[7], (DEC_BATCH, D_MODEL), 1.0),
        'w_ada': nrm(ks[8], (DEPTH, D_MODEL, 3 * N_SUB * D_MODEL), s),
        'b_ada': nrm(ks[9], (DEPTH, 3 * N_SUB * D_MODEL), 0.02),
        'norm_pre': 1.0 + nrm(ks[10], (DEPTH, N_SUB, D_MODEL), 0.02),
        'norm_post': 1.0 + nrm(ks[11], (DEPTH, N_SUB, D_MODEL), 0.02),
        'ffn_w1': nrm(ks[12], (DEPTH, 2, D_MODEL, D_FF), s),
        'ffn_w3': nrm(ks[13], (DEPTH, 2, D_MODEL, D_FF), s),
        'ffn_w2': nrm(ks[14], (DEPTH, 2, D_FF, D_MODEL), D_FF ** -0.5),
        'w_in': nrm(ks[15], (DEPTH, D_MODEL, D_IN), s),
        'w_out': nrm(ks[16], (DEPTH, D_MIX, D_MODEL), D_MIX ** -0.5),
        'w_conv_a': nrm(ks[17], (DEPTH, CONV_A, D_A), CONV_A ** -0.5),
        'w_conv_b': nrm(ks[18], (DEPTH, CONV_B, D_B), CONV_B ** -0.5),
        'b_conv_b': nrm(ks[19], (DEPTH, D_B), 0.02),
        'w_gate_r': nrm(ks[20], (DEPTH, HEADS_B, HEAD_DIM_B, HEAD_DIM_B), HEAD_DIM_B ** -0.5),
        'b_gate_r': nrm(ks[22], (DEPTH, D_B), 0.02),
        'w_gate_i': nrm(ks[23], (DEPTH, HEADS_B, HEAD_DIM_B, HEAD_DIM_B), HEAD_DIM_B ** -0.5),
        'b_gate_i': nrm(ks[24], (DEPTH, D_B), 0.02),
        'rg_lambda': jnp.log(u_lam) - jnp.log1p(-u_lam),
        'w_conv_c': nrm(ks[25], (DEPTH, CONV_C, D_C), CONV_C ** -0.5),
        'b_conv_c': nrm(ks[26], (DEPTH, D_C), 0.02),
        'ln_c_g': 1.0 + nrm(ks[27], (DEPTH, D_C), 0.02),
        'ln_c_b': nrm(ks[28], (DEPTH, D_C), 0.02),
        'grp_g': 1.0 + nrm(ks[29], (DEPTH, D_MIX), 0.02),
    }


def reference(x_prompt, x_sample, state_conv_a, state_conv_b, state_rglru, state_conv_c,
              c_prompt, c_sample, w_ada, b_ada, norm_pre, norm_post, ffn_w1, ffn_w3, ffn_w2,
              w_in, w_out, w_conv_a, w_conv_b, b_conv_b, w_gate_r, b_gate_r, w_gate_i, b_gate_i,
              rg_lambda, w_conv_c, b_conv_c, ln_c_g, ln_c_b, grp_g):

    def trunk(x, c, st_a, st_b, st_h, st_c):
        bsz = x.shape[0]
        new_a, new_b, new_h, new_c = [], [], [], []
        for l in range(DEPTH):
            mod = (jax.nn.silu(c) @ w_ada[l] + b_ada[l]).reshape(bsz, 3 * N_SUB, 1, D_MODEL).astype(x.dtype)
            h = rmsnorm(x, norm_pre[l, 0]) * (1.0 + mod[:, 1]) + mod[:, 0]
            f = swiglu(h, ffn_w1[l, 0], ffn_w3[l, 0], ffn_w2[l, 0])
            x = x + 0.5 * mod[:, 2] * rmsnorm(f, norm_post[l, 0])
            h = rmsnorm(x, norm_pre[l, 1]) * (1.0 + mod[:, 4]) + mod[:, 3]
            m, na, nb, nh, nc = token_mixer(h, st_a[l], st_b[l], st_h[l], st_c[l], w_in[l], w_out[l],
                                            w_conv_a[l], w_conv_b[l], b_conv_b[l], w_gate_r[l], b_gate_r[l],
                                            w_gate_i[l], b_gate_i[l], rg_lambda[l], w_conv_c[l], b_conv_c[l],
                                            ln_c_g[l], ln_c_b[l], grp_g[l])
            x = x + mod[:, 5] * rmsnorm(m, norm_post[l, 1])
            h = rmsnorm(x, norm_pre[l, 2]) * (1.0 + mod[:, 7]) + mod[:, 6]
            f = swiglu(h, ffn_w1[l, 1], ffn_w3[l, 1], ffn_w2[l, 1])
            x = x + 0.5 * mod[:, 8] * rmsnorm(f, norm_post[l, 2])
            new_a.append(na)
            new_b.append(nb)
            new_h.append(nh)
            new_c.append(nc)
        return x, jnp.stack(new_a), jnp.stack(new_b), jnp.stack(new_h), jnp.stack(new_c)

    dt = x_prompt.dtype
    p_a0 = jnp.zeros((DEPTH, BATCH, CONV_A - 1, D_A), dt)
    p_b0 = jnp.zeros((DEPTH, BATCH, CONV_B - 1, D_B), dt)
    p_h0 = jnp.zeros((DEPTH, BATCH, D_B), dt)
    p_c0 = jnp.zeros((DEPTH, BATCH, CONV_C - 1, D_C), dt)
    y_prompt, pa, pb, ph, pc = trunk(x_prompt, c_prompt, p_a0, p_b0, p_h0, p_c0)
    y_sample, sa, sb, sh, sc = trunk(x_sample, c_sample, state_conv_a, state_conv_b, state_rglru, state_conv_c)
    return (y_prompt, y_sample, pa, pb, ph, pc, sa, sb, sh, sc)
```

```python
import numpy as np
import concourse.bass as bass
import concourse.mybir as mybir
from concourse.bass_utils import run_bass_kernel_spmd

F32 = mybir.dt.float32
BF16 = mybir.dt.bfloat16
AF = mybir.ActivationFunctionType
ALU = mybir.AluOpType

import os
NCORES = 8
L = int(os.environ.get("KDEPTH", "4"))
KSTOP = int(os.environ.get("KSTOP", "1000"))
KINIT = int(os.environ.get("KINIT", "3"))
KSUB = int(os.environ.get("KSUB", "99"))
KDUMMY = int(os.environ.get("KDUMMY", "0"))
NFC = 9 * L
NMW = NFC * 9
D = 1024
DFF = 2816
NF = 22
SEG = 1024
NS = 16
NT = SEG + NS
HAL = 30
MW = HAL + SEG + HAL + NS
P0, P1 = HAL, HAL + SEG
S0, S1 = P1 + HAL, P1 + HAL + NS
CT = [(0, 512), (512, 512), (SEG, NS)]
MT = [(P0, 512), (P0 + 512, 512), (S0, NS)]
NARR = 19
EPS = 1e-6
NPIECE = 57
GELU_C = 1.5957691216057308

_SP = {}
_off = 0
for _n, _sz in [("npre", L * 3 * 8), ("npost", L * 3 * 8), ("wca", L * 2 * 3), ("wcb", L * 4 * 4),
                ("bcb", L * 4), ("bgr", L * 4), ("bgi", L * 4), ("lam", L * 4), ("wcc", L * 2 * 31),
                ("bcc", L * 2), ("lng", L * 2), ("lnb", L * 2), ("grp", L * 8),
                ("pm", 9), ("lt", 8), ("smask", 8)]:
    _SP[_n] = (_off, _sz)
    _off += _sz
NSP = _off


class Prog:
    ENG = ["pe", "act", "dve", "pool", "sp"]

    def __init__(self):
        self.streams = {e: [] for e in self.ENG}
        self.cnt = {e: 0 for e in self.ENG}
        self.dcnt = {}
        self.waited = {e: {} for e in self.ENG}
        self.lw = {}
        self.rd = {}

    def _deps(self, eng, reads, writes):
        evs = {}

        def add(k, v):
            if evs.get(k, 0) < v:
                evs[k] = v
        for r in reads:
            ev = self.lw.get(r)
            if ev is not None:
                add(*ev)
        for w in writes:
            ev = self.lw.get(w)
            if ev is not None:
                add(*ev)
            for k, v in self.rd.get(w, {}).items():
                add(k, v)
        out = []
        wd = self.waited[eng]
        for k, v in evs.items():
            if wd.get(k, 0) < v:
                wd[k] = v
                out.append((k, v))
        return out

    def _commit(self, ev, reads, writes):
        k, v = ev
        for r in reads:
            d = self.rd.setdefault(r, {})
            if d.get(k, 0) < v:
                d[k] = v
        for w in writes:
            self.lw[w] = ev
            self.rd[w] = {}

    def op(self, eng, fn, reads=(), writes=()):
        waits = self._deps(eng, reads, writes)
        self.cnt[eng] += 1
        ev = (eng, self.cnt[eng])
        self.streams[eng].append((waits, fn, ev, 1))
        self._commit(ev, reads, writes)
        return ev

    def dma(self, eng, fn, sem, reads=(), writes=(), inc=16):
        waits = self._deps(eng, reads, writes)
        self.dcnt[sem] = self.dcnt.get(sem, 0) + inc
        ev = (sem, self.dcnt[sem])
        self.streams[eng].append((waits, fn, ev, inc))
        self._commit(ev, reads, writes)
        return ev

    def barrier(self):
        for e in self.ENG:
            wd = self.waited[e]
            waits = []
            for o in self.ENG:
                if self.cnt[o] > wd.get(o, 0):
                    wd[o] = self.cnt[o]
                    waits.append((o, self.cnt[o]))
            for k, v in self.dcnt.items():
                if v > wd.get(k, 0):
                    wd[k] = v
                    waits.append((k, v))
            if waits:
                self.streams[e].append((waits, None, None, 0))


def build_program():
    nc = bass.Bass("TRN2", target_bir_lowering=False)
    pr = Prog()

    def din(name, shape):
        return nc.dram_tensor(name, list(shape), F32, kind="ExternalInput").ap()

    def dout(name, shape):
        return nc.dram_tensor(name, list(shape), F32, kind="ExternalOutput").ap()

    xin = din("xin", [2, 128, 8, SEG])
    xs_in = din("xs", [128, 8, NS])
    ws_d = din("ws", [L, NPIECE, 128, 2048])
    wb_d = din("wb", [L, 2, 8, 128, 2, 1408])
    wada_d = din("wada", [NFC // 3, 128, 3 * 8 * 128])
    bada_d = din("bada", [128, NMW])
    ct_d = din("ct", [128, 72])
    spar_d = din("spar", [128, NSP])
    gw_d = din("gw", [L, 128, 2 * 4 * 128])
    ident_d = din("ident", [128, 128])
    states_d = din("states", [128, L * 80])
    if KDUMMY:
        din("dummyin", [KDUMMY * 1024, 256])

    yt_d = dout("yt", [2, 128, 8, SEG])
    ys_d = dout("ys", [128, 8, NS])
    tails_d = dout("tails", [2, L, 128, 80])

    ibm = nc.dram_tensor("ibm", [128, NMW], F32)
    obm = nc.dram_tensor("obm", [NCORES * 128, NMW], F32)
    _ib1 = nc.dram_tensor("ib1", [128, 76], F32)
    _ob1 = nc.dram_tensor("ob1", [NCORES * 128, 76], F32)
    _ib2 = nc.dram_tensor("ib2", [128, 8], F32)
    _ob2 = nc.dram_tensor("ob2", [NCORES * 128, 8], F32)
    ib1 = [_ib1] * (2 * L)
    ob1 = [_ob1] * (2 * L)
    ib2 = [_ib2] * (2 * L)
    ob2 = [_ob2] * (2 * L)

    from contextlib import ExitStack
    es = ExitStack()

    def sb(name, shape, dt=F32):
        return es.enter_context(nc.sbuf_tensor(name, list(shape), dt))

    x = sb("x", [128, 8, NT])
    h = sb("h", [128, 8, NT], BF16)
    S = sb("S", [128, NARR * MW])
    wa = sb("wa", [128, 3, 2, 8, 128], BF16)
    wbt = sb("wbt", [128, 3, NF, 128], BF16)
    dgb = sb("dgb", [128, 62, 128], BF16)
    tmpa = sb("tmpa", [128, 2, 512])
    tmpb = sb("tmpb", [128, 2, 512])
    rsd = sb("rsd", [128, 2, 512])
    spar = sb("spar_sb", [128, NSP])
    gw = sb("gw_sb", [128, 2 * 4 * 128], BF16)
    ident = sb("ident_sb", [128, 128], BF16)
    ones = sb("ones_sb", [128, 128], BF16)
    cst = sb("cst", [128, 4])
    states = sb("states_sb", [128, L * 80])
    ct = sb("ct_sb", [128, 72])
    sct = sb("sct_sb", [128, 72])
    mods2 = sb("mods2", [128, 2, 8 * NFC])
    modA = sb("modA", [128, 2, L * 3 * 8])
    modSh = sb("modSh", [128, 2, L * 3 * 8])
    modG = sb("modG", [128, 2, L * 3 * 8])
    s8 = sb("s8", [128, 3, L * 4])
    stg1 = sb("stg1", [128, 2, 80])
    g1 = sb("g1", [128, 8, 76])
    g1save = sb("g1save", [128, L, 76])
    hal = sb("hal", [128, 76])
    stg2 = sb("stg2", [128, 8])
    g2 = sb("g2", [128, 8, 8])
    g2save = sb("g2save", [128, L, 8, 8])
    chA = sb("chA", [128, 4, 16])
    chB = sb("chB", [128, 4, 16])
    chO = sb("chO", [128, 4, 16])
    hin = sb("hin", [128, 4])

    psb = [es.enter_context(nc.psum_tensor(f"ps{i}", [128, 512], F32)) for i in range(8)]

    def M(i):
        return S[:, i * MW:(i + 1) * MW]
    g_v = S[:, 0:NF * NT // 2].bitcast(BF16).rearrange("p (f t) -> p f t", t=NT)
    f_v = S[:, NF * NT // 2: NF * NT // 2 + 8 * NT].rearrange("p (m t) -> p m t", t=NT)

    def Mb(i):
        return M(i).bitcast(BF16).rearrange("p (j t) -> p j t", t=MW)

    def spv(name):
        o, n = _SP[name]
        return spar[:, o:o + n]

    def spc(name, idx):
        o, n = _SP[name]
        return spar[:, o + idx:o + idx + 1]

    ONE = cst[:, 0:1]
    EPSC = cst[:, 1:2]

    def act(out, in_, func, bias=None, scale=None, reads=(), writes=()):
        kw = {}
        if bias is not None:
            kw["bias"] = bias
        if scale is not None:
            kw["scale"] = scale
        return pr.op("act", lambda e: e.activation(out=out, in_=in_, func=func, **kw), reads, writes)

    def tt(eng, out, a, b, op, reads=(), writes=()):
        return pr.op(eng, lambda e: e.tensor_tensor(out=out, in0=a, in1=b, op=op), reads, writes)

    def ts(eng, out, a, s1, s2, op0, op1=None, reads=(), writes=()):
        if op1 is None:
            return pr.op(eng, lambda e: e.tensor_scalar(out=out, in0=a, scalar1=s1, scalar2=None, op0=op0),
                         reads, writes)
        return pr.op(eng, lambda e: e.tensor_scalar(out=out, in0=a, scalar1=s1, scalar2=s2, op0=op0, op1=op1),
                     reads, writes)

    def stt(eng, out, a, s, b, op0, op1, reads=(), writes=()):
        return pr.op(eng, lambda e: e.scalar_tensor_tensor(out=out, in0=a, scalar=s, in1=b, op0=op0, op1=op1),
                     reads, writes)

    def cp(eng, out, in_, reads=(), writes=()):
        return pr.op(eng, lambda e: e.tensor_copy(out=out, in_=in_), reads, writes)

    def recip(out, in_, reads=(), writes=()):
        return pr.op("dve", lambda e: e.reciprocal(out=out, in_=in_), reads, writes)

    def mset(eng, ap, val, reads=(), writes=()):
        return pr.op(eng, lambda e: e.memset(ap, val), reads, writes)

    def mm_group(mms, reads=(), writes=()):
        def fn(e):
            inst = None
            n = len(mms)
            for i, (o, lt_, rh, st, sp_) in enumerate(mms):
                inst = e.matmul(o, lt_, rh, start=st, stop=sp_)
            return inst
        return pr.op("pe", fn, reads, writes)

    def dma(eng, out, in_, sem, reads=(), writes=(), maxb=None):
        if maxb is None:
            return pr.dma(eng, lambda e: e.dma_start(out=out, in_=in_), sem, reads, writes)
        return pr.dma(eng, lambda e: e.dma_start(out=out, in_=in_, max_dma_last_dim=maxb), sem, reads, writes)

    bank_state = {"gen": 0, "st": 0}

    def gbank():
        b = bank_state["gen"] % 6
        bank_state["gen"] += 1
        return b

    def sbank():
        b = 6 + bank_state["st"] % 2
        bank_state["st"] += 1
        return b

    tmp_state = {"a": 0, "b": 0, "r": 0}

    def nxt(k):
        v = tmp_state[k] % 2
        tmp_state[k] += 1
        return v

    wa_use = {"n": 0}
    wb_use = {"n": 0}
    wa_sched = []
    U_ORDER = [1, 2, 3, 4, 7, 8, 0, 5, 6]
    for ph_ in range(2):
        for l_ in range(L):
            for q_ in list(range(22)) + [22 + u for u in U_ORDER] + list(range(31, NPIECE)):
                wa_sched.append((l_, q_))
    wb_sched = []
    for ph_ in range(2):
        for l_ in range(L):
            for e_ in range(2):
                for m_ in range(8):
                    wb_sched.append((l_, e_, m_))
    wa_issued = {"n": 0}
    wb_issued = {"n": 0}

    def issue_wa():
        i = wa_issued["n"]
        if i >= len(wa_sched):
            return
        l_, q_ = wa_sched[i]
        slot = i % 3
        dma("pool", wa[:, slot].rearrange("p s k j -> p (s k j)"), ws_d[l_, q_], f"wa{slot}",
            writes=[("wa", slot)])
        wa_issued["n"] += 1

    def issue_wb():
        i = wb_issued["n"]
        if i >= len(wb_sched):
            return
        l_, e_, m_ = wb_sched[i]
        slot = i % 3
        dma("pool", wbt[:, slot].rearrange("p (a f) j -> p a (f j)", a=2), wb_d[l_, e_, m_], f"wb{slot}",
            writes=[("wb", slot)])
        wb_issued["n"] += 1

    def next_wa():
        i = wa_use["n"]
        wa_use["n"] += 1
        assert i < wa_issued["n"]
        return i % 3

    def next_wb():
        i = wb_use["n"]
        wb_use["n"] += 1
        assert i < wb_issued["n"]
        return i % 3

    dma("sp", spar[:, :], spar_d, "ld_spar", writes=["spar"])
    dma("sp", ct[:, :], ct_d, "ld_ct", writes=["ct"])
    dma("sp", states[:, :], states_d, "ld_st", writes=["states"])
    dma("pool", ident[:, :], ident_d, "ld_id", writes=["ident"])
    mset("dve", ones[:, :], 1.0, writes=["ones"])
    mset("dve", cst[:, 0:1], 1.0, writes=["cst"])
    mset("dve", cst[:, 1:2], EPS, writes=["cst"])
    mset("dve", cst[:, 2:4], 0.0, writes=["cst"])
    for c in range(8):
        pass
    dma("sp", x[:, :, 0:SEG], xin[0], "ldx", writes=[("x", c, t) for c in range(8) for t in range(3)])
    dma("sp", x[:, :, SEG:NT], xs_in, "ldxs", writes=[("x", c, 2) for c in range(8)])

    def init_s8():
      act(s8[:, 0, :], spv("lam"), AF.Exp, scale=-1.0, reads=["spar"], writes=["s8t"])
      ts("dve", s8[:, 0, :], s8[:, 0, :], 1.0, None, ALU.add, reads=["s8t"], writes=["s8t"])
      act(s8[:, 0, :], s8[:, 0, :], AF.Ln, reads=["s8t"], writes=["s8t"])
      ts("dve", s8[:, 1, :], s8[:, 0, :], -8.0, None, ALU.mult, reads=["s8t"], writes=["s8"])
      ts("dve", s8[:, 2, :], s8[:, 0, :], -16.0, None, ALU.mult, reads=["s8t"], writes=["s8"])

    if KINIT >= 1:
        init_s8()

    def init_mod():
      if True:
        act(sct[:, :], ct[:, :], AF.Silu, reads=["ct"], writes=["sct"])
        wad = S[:, 0:2 * 3072].rearrange("p (s n) -> p s n", s=2)
        mms_bank = psb[0]
        for i in range(NFC // 3):
            slot = i % 2
            dma("sp", wad[:, slot, :], wada_d[i], f"wad{slot}", writes=[("wad", slot)])
            wv = wad[:, slot, :].rearrange("p (f k j) -> p f k j", f=3, k=8)
            mms = []
            for fl in range(3):
                fc = i * 3 + fl
                for kc in range(8):
                    mms.append((mms_bank[:, fc * 9:(fc + 1) * 9], wv[:, fl, kc, :], sct[:, kc * 9:(kc + 1) * 9],
                                kc == 0, kc == 7))
            mm_group(mms, reads=[("wad", slot), "sct"], writes=[("ps", 0)])
        modstg = S[:, 12288:12288 + NMW]
        modall = S[:, 12800:12800 + 8 * NMW]
        dma("sp", modall[:, 0:NMW], bada_d, "ld_bada", writes=["bada"])
        tt("dve", modstg, mms_bank[:, 0:NMW], modall[:, 0:NMW], ALU.add, reads=[("ps", 0), "bada"], writes=["modstg"])
        if KINIT < 3:
            return
        dma("sp", ibm.ap()[:, :], modstg, "ccio", reads=["modstg"], writes=["ibm"])
        pr.dma("pool", lambda e: e.collective_compute("AllGather", ALU.bypass,
                                                       replica_groups=[list(range(NCORES))],
                                                       ins=[ibm.ap().opt()], outs=[obm.ap().opt()]),
               "ccx", reads=["ibm"], writes=["obm"], inc=1)
        dma("sp", modall.rearrange("p (r w) -> p r w", r=8), obm.ap().rearrange("(r p) w -> p r w", p=128), "ccio",
            reads=["obm", "bada", "modstg"], writes=["modall"])
        mview = modall.rearrange("p (g b) -> p g b", b=9)
        cp("dve", mods2[:, 0, :], mview[:, :, 0], reads=["modall"], writes=["mods2"])
        ts("dve", mods2[:, 1, :], mview[:, :, 1], spc("smask", 0), None, ALU.mult, reads=["modall", "spar"],
           writes=["mods2"])
        for b in range(1, 8):
            stt("dve", mods2[:, 1, :], mview[:, :, 1 + b], spc("smask", b), mods2[:, 1, :], ALU.mult, ALU.add,
                reads=["modall", "mods2"], writes=["mods2"])
        for bi in range(2):
            v5 = mods2[:, bi, :].rearrange("p (ls t d) -> p ls t d", t=3, d=8)
            npre_v = spv("npre").rearrange("p (ls d) -> p ls d", d=8)
            npost_v = spv("npost").rearrange("p (ls d) -> p ls d", d=8)
            mA = modA[:, bi, :].rearrange("p (ls d) -> p ls d", d=8)
            mS = modSh[:, bi, :].rearrange("p (ls d) -> p ls d", d=8)
            mG = modG[:, bi, :].rearrange("p (ls d) -> p ls d", d=8)
            stt("dve", mA, v5[:, :, 1, :], 1.0, npre_v, ALU.add, ALU.mult, reads=["mods2", "spar"], writes=["modA"])
            cp("dve", mS, v5[:, :, 0, :], reads=["mods2"], writes=["modSh"])
            tt("dve", mG, v5[:, :, 2, :], npost_v, ALU.mult, reads=["mods2", "spar"], writes=["modG"])
            mG4 = modG[:, bi, :].rearrange("p (l s d) -> p l s d", s=3, d=8)
            for s_ in (0, 2):
                ts("dve", mG4[:, :, s_, :], mG4[:, :, s_, :], 0.5, None, ALU.mult, reads=["modG"], writes=["modG"])

    if KINIT >= 2:
        init_mod()
    pr.barrier()

    def mcol(arr, bi, l, s, c):
        o = (l * 3 + s) * 8 + c
        return arr[:, bi, o:o + 1]

    for _ in range(3):
        issue_wa()

    def rstd_from(rhs_list, nfeat, n, reads):
        b = sbank()
        mms = [(psb[b][:, 0:n], ones[:, :], r, i == 0, i == len(rhs_list) - 1) for i, r in enumerate(rhs_list)]
        mm_group(mms, reads=list(reads) + ["ones"], writes=[("ps", b)])
        k = nxt("r")
        o = rsd[:, k, 0:n]
        act(o, psb[b][:, 0:n], AF.Sqrt, bias=EPSC, scale=1.0 / nfeat, reads=[("ps", b), "cst"], writes=[("rsd", k)])
        recip(o, o, reads=[("rsd", k)], writes=[("rsd", k)])
        return o, ("rsd", k)

    def ffn(l, e, ph):
        s = 0 if e == 0 else 2
        ntile = 3 if ph == 0 else 2
        issue_wb()
        issue_wb()
        for ti in range(ntile):
            c0, n = CT[ti]
            bi = 1 if ti == 2 else 0
            act(h[:, :, c0:c0 + n], x[:, :, c0:c0 + n], AF.Square, reads=[("x", c, ti) for c in range(8)],
                writes=[("h", c, ti) for c in range(8)])
            r_ap, r_res = rstd_from([h[:, c, c0:c0 + n] for c in range(8)], D, n, [("h", c, ti) for c in range(8)])
            for c in range(8):
                k = nxt("a")
                stt("dve", tmpa[:, k, 0:n], x[:, c, c0:c0 + n], mcol(modA, bi, l, s, c), r_ap, ALU.mult, ALU.mult,
                    reads=[("x", c, ti), r_res, "modA"], writes=[("tmpa", k)])
                act(h[:, c, c0:c0 + n], tmpa[:, k, 0:n], AF.Identity, bias=mcol(modSh, bi, l, s, c),
                    reads=[("tmpa", k), "modSh"], writes=[("h", c, ti)])
        if KSUB < 2:
            for f in range(NF):
                next_wa(); issue_wa()
            for m in range(8):
                next_wb()
                if m + 2 < 8:
                    issue_wb()
            return
        for f in range(NF):
            slot = next_wa()
            for ti in range(ntile):
                c0, n = CT[ti]
                b1 = ((f * ntile + ti) % 4) * 2
                b3 = b1 + 1
                mms = []
                for kc in range(8):
                    mms.append((psb[b1][:, 0:n], wa[:, slot, 0, kc, :], h[:, kc, c0:c0 + n], kc == 0, kc == 7))
                for kc in range(8):
                    mms.append((psb[b3][:, 0:n], wa[:, slot, 1, kc, :], h[:, kc, c0:c0 + n], kc == 0, kc == 7))
                mm_group(mms, reads=[("wa", slot)] + [("h", c, ti) for c in range(8)],
                         writes=[("ps", b1), ("ps", b3)])
                k = nxt("b")
                act(tmpb[:, k, 0:n], psb[b1][:, 0:n], AF.Silu, reads=[("ps", b1)], writes=[("tmpb", k)])
                tt("dve", g_v[:, f, c0:c0 + n], tmpb[:, k, 0:n], psb[b3][:, 0:n], ALU.mult,
                   reads=[("tmpb", k), ("ps", b3)], writes=[("g", f, ti)])
            if e == 0:
                for di in range(f * 3, min(62, f * 3 + 3)):
                    act(dgb[:, di, :], ident[:, :], AF.Identity, scale=spc("wcc", l * 62 + di),
                        reads=["ident", "spar"], writes=[("dg", di)])
            issue_wa()
        if KSUB < 3:
            for m in range(8):
                next_wb()
                if m + 2 < 8:
                    issue_wb()
            return
        for m in range(8):
            slot = next_wb()
            for ti in range(ntile):
                c0, n = CT[ti]
                b = 4 + (m * 3 + ti) % 2
                mms = [(psb[b][:, 0:n], wbt[:, slot, fc, :], g_v[:, fc, c0:c0 + n], fc == 0, fc == NF - 1)
                       for fc in range(NF)]
                mm_group(mms, reads=[("wb", slot)] + [("g", fc, ti) for fc in range(NF)], writes=[("ps", b)])
                act(f_v[:, m, c0:c0 + n], psb[b][:, 0:n], AF.Copy, reads=[("ps", b)], writes=[("f", m, ti)])
                act(h[:, m, c0:c0 + n], f_v[:, m, c0:c0 + n], AF.Square, reads=[("f", m, ti)], writes=[("h", m, ti)])
            if m + 2 < 8:
                issue_wb()
        if KSUB < 4:
            return
        for ti in range(ntile):
            c0, n = CT[ti]
            bi = 1 if ti == 2 else 0
            r_ap, r_res = rstd_from([h[:, c, c0:c0 + n] for c in range(8)], D, n, [("h", c, ti) for c in range(8)])
            for c in range(8):
                k = nxt("a")
                stt("dve", tmpa[:, k, 0:n], f_v[:, c, c0:c0 + n], mcol(modG, bi, l, s, c), r_ap, ALU.mult, ALU.mult,
                    reads=[("f", c, ti), r_res, "modG"], writes=[("tmpa", k)])
                tt("dve", x[:, c, c0:c0 + n], x[:, c, c0:c0 + n], tmpa[:, k, 0:n], ALU.add,
                   reads=[("tmpa", k), ("x", c, ti)], writes=[("x", c, ti)])

    def mixer(l, ph):
        pl = ph * L + l
        ntile = 3 if ph == 0 else 2
        WE = S1 if ph == 0 else P1
        FW = slice(P0, WE)
        s = 1
        A_AB, A_PR, A_BX, A_V, A_BG, A_VB = 0, 2, 4, 8, 10, 14
        T1, T2, A_R, A_Q, A_Y, A_T3 = 15, 16, 17, 18, 2, 0
        ar = lambda i: ("M", i)
        for ti in range(ntile):
            c0, n = CT[ti]
            bi = 1 if ti == 2 else 0
            act(h[:, :, c0:c0 + n], x[:, :, c0:c0 + n], AF.Square, reads=[("x", c, ti) for c in range(8)],
                writes=[("h", c, ti) for c in range(8)])
            r_ap, r_res = rstd_from([h[:, c, c0:c0 + n] for c in range(8)], D, n, [("h", c, ti) for c in range(8)])
            for c in range(8):
                k = nxt("a")
                stt("dve", tmpa[:, k, 0:n], x[:, c, c0:c0 + n], mcol(modA, bi, l, s, c), r_ap, ALU.mult, ALU.mult,
                    reads=[("x", c, ti), r_res, "modA"], writes=[("tmpa", k)])
                act(h[:, c, c0:c0 + n], tmpa[:, k, 0:n], AF.Identity, bias=mcol(modSh, bi, l, s, c),
                    reads=[("tmpa", k), "modSh"], writes=[("h", c, ti)])
        dma("pool", gw[:, :], gw_d[l], "ld_gw", writes=["gw"])
        if ph == 0:
            so = l * 80
            for j in range(2):
                cp("dve", M(A_PR + j)[:, S0 - 2:S0], states[:, so + j * 2: so + j * 2 + 2], reads=["states"],
                   writes=[ar(A_PR + j)])
            for j in range(4):
                cp("dve", M(A_BX + j)[:, S0 - 3:S0], states[:, so + 4 + j * 3: so + 4 + j * 3 + 3],
                   reads=["states"], writes=[ar(A_BX + j)])
            for j in range(2):
                cp("dve", M(A_V + j)[:, S0 - 30:S0], states[:, so + 16 + j * 30: so + 16 + j * 30 + 30],
                   reads=["states"], writes=[ar(A_V + j)])
        def publish_x1():
            tails_into(stg1[:, 0, :], P1)
            if ph == 0:
                tails_into(stg1[:, 1, :], S1)
            dma("sp", ib1[pl].ap()[:, :], stg1[:, 0, 0:76], "ccio", reads=["stg1"], writes=[("ib1", 0)])
            pr.dma("pool", lambda e: e.collective_compute("AllGather", ALU.bypass,
                                                           replica_groups=[list(range(NCORES))],
                                                           ins=[ib1[pl].ap().opt()], outs=[ob1[pl].ap().opt()]),
                   "ccx", reads=[("ib1", 0)], writes=[("ob1", 0)], inc=1)
            dma("sp", g1[:, :, :], ob1[pl].ap().rearrange("(r p) w -> p r w", p=128), "ccio",
                reads=[("ob1", 0)], writes=["g1"])

        def tails_into(dst, pe_, reads_extra=()):
            for j in range(2):
                cp("dve", dst[:, j * 2:j * 2 + 2], M(A_PR + j)[:, pe_ - 2:pe_], reads=[ar(A_PR + j)], writes=["stg1"])
            for j in range(4):
                cp("dve", dst[:, 4 + j * 3:4 + j * 3 + 3], M(A_BX + j)[:, pe_ - 3:pe_], reads=[ar(A_BX + j)],
                   writes=["stg1"])
            for j in range(2):
                cp("dve", dst[:, 16 + j * 30:16 + j * 30 + 30], M(A_V + j)[:, pe_ - 30:pe_], reads=[ar(A_V + j)],
                   writes=["stg1"])

        for qi, q in enumerate(U_ORDER):
            slot = next_wa()
            for sidx in range(2):
                uc = 2 * q + sidx
                for ti in range(ntile):
                    c0, n = CT[ti]
                    m0 = MT[ti][0]
                    b = gbank()
                    mms = [(psb[b][:, 0:n], wa[:, slot, sidx, kc, :], h[:, kc, c0:c0 + n], kc == 0, kc == 7)
                           for kc in range(8)]
                    mm_group(mms, reads=[("wa", slot)] + [("h", c, ti) for c in range(8)], writes=[("ps", b)])
                    ps = psb[b][:, 0:n]
                    R = [("ps", b)]
                    if uc < 2:
                        act(M(A_AB + uc)[:, m0:m0 + n], ps, AF.Copy, reads=R, writes=[ar(A_AB + uc)])
                    elif uc < 4:
                        act(M(A_PR + uc - 2)[:, m0:m0 + n], ps, AF.Copy, reads=R, writes=[ar(A_PR + uc - 2)])
                    elif uc < 6:
                        j = uc - 4
                        tt("dve", M(A_PR + j)[:, m0:m0 + n], M(A_PR + j)[:, m0:m0 + n], ps, ALU.mult,
                           reads=R + [ar(A_PR + j)], writes=[ar(A_PR + j)])
                    elif uc < 10:
                        act(M(A_BX + uc - 6)[:, m0:m0 + n], ps, AF.Copy, reads=R, writes=[ar(A_BX + uc - 6)])
                    elif uc < 14:
                        act(M(A_BG + uc - 10)[:, m0:m0 + n], ps, AF.Copy, reads=R, writes=[ar(A_BG + uc - 10)])
                    elif uc < 16:
                        act(M(A_V + uc - 14)[:, m0:m0 + n], ps, AF.Copy, reads=R, writes=[ar(A_V + uc - 14)])
                    else:
                        j = uc - 16
                        k = nxt("b")
                        act(tmpb[:, k, 0:n], ps, AF.Sigmoid, reads=R, writes=[("tmpb", k)])
                        tt("dve", M(A_V + j)[:, m0:m0 + n], M(A_V + j)[:, m0:m0 + n], tmpb[:, k, 0:n], ALU.mult,
                           reads=[("tmpb", k), ar(A_V + j)], writes=[ar(A_V + j)])
            issue_wa()
        publish_x1()
        dg = dgb
        for j in range(4):
            bg = M(A_BG + j)[:, FW]
            t1 = M(T1)[:, FW]
            act(t1, bg, AF.Square, reads=[ar(A_BG + j)], writes=[ar(T1)])
            ts("dve", t1, t1, 0.044715, 1.0, ALU.mult, ALU.add, reads=[ar(T1)], writes=[ar(T1)])
            tt("dve", t1, t1, bg, ALU.mult, reads=[ar(T1), ar(A_BG + j)], writes=[ar(T1)])
            act(t1, t1, AF.Sigmoid, scale=GELU_C, reads=[ar(T1)], writes=[ar(T1)])
            tt("dve", bg, bg, t1, ALU.mult, reads=[ar(T1), ar(A_BG + j)], writes=[ar(A_BG + j)])
        ts("dve", hal[:, :], g1[:, 0, :], spc("pm", 0), None, ALU.mult, reads=["g1", "spar"], writes=["hal"])
        for r in range(1, 8):
            stt("dve", hal[:, :], g1[:, r, :], spc("pm", r), hal[:, :], ALU.mult, ALU.add, reads=["g1", "hal"],
                writes=["hal"])
        if ph == 0:
            cp("dve", g1save[:, l, :], g1[:, 7, :], reads=["g1"], writes=["g1save"])
        else:
            stt("dve", hal[:, :], g1save[:, l, :], spc("pm", 8), hal[:, :], ALU.mult, ALU.add,
                reads=["g1save", "hal"], writes=["hal"])
        for j in range(2):
            cp("dve", M(A_PR + j)[:, P0 - 2:P0], hal[:, j * 2:j * 2 + 2], reads=["hal"], writes=[ar(A_PR + j)])
        for j in range(4):
            cp("dve", M(A_BX + j)[:, P0 - 3:P0], hal[:, 4 + j * 3:4 + j * 3 + 3], reads=["hal"],
               writes=[ar(A_BX + j)])
        for j in range(2):
            cp("dve", M(A_V + j)[:, 0:P0], hal[:, 16 + j * 30:16 + j * 30 + 30], reads=["hal"], writes=[ar(A_V + j)])
        vb = Mb(A_VB)
        for j in range(2):
            act(vb[:, j, 0:WE], M(A_V + j)[:, 0:WE], AF.Copy, reads=[ar(A_V + j)], writes=[ar(A_VB)])

        def group_norm(src_arrays, ych0, sq_arr):
            nchk = len(src_arrays)
            sqv = [Mb(a) for a in sq_arr]
            for i, a in enumerate(src_arrays):
                act(sqv[i // 2][:, i % 2, FW], M(a)[:, FW], AF.Square, reads=[ar(a)], writes=[ar(sq_arr[i // 2])])
            for ti in range(ntile):
                m0, n = MT[ti]
                c0 = CT[ti][0]
                r_ap, r_res = rstd_from([sqv[i // 2][:, i % 2, m0:m0 + n] for i in range(nchk)], nchk * 128, n,
                                        [ar(q_) for q_ in sq_arr])
                for i, a in enumerate(src_arrays):
                    stt("dve", h[:, ych0 + i, c0:c0 + n], M(a)[:, m0:m0 + n], spc("grp", l * 8 + ych0 + i), r_ap,
                        ALU.mult, ALU.mult, reads=[ar(a), r_res, "spar"], writes=[("h", ych0 + i, ti)])

        for j in range(2):
            t1 = M(T1)
            w = lambda kk: spc("wca", (l * 2 + j) * 3 + kk)
            prj = M(A_PR + j)
            ts("dve", t1[:, FW], prj[:, P0 - 2:WE - 2], w(0), None, ALU.mult, reads=[ar(A_PR + j), "spar"],
               writes=[ar(T1)])
            stt("dve", t1[:, FW], prj[:, P0 - 1:WE - 1], w(1), t1[:, FW], ALU.mult, ALU.add,
                reads=[ar(A_PR + j), ar(T1)], writes=[ar(T1)])
            stt("dve", t1[:, FW], prj[:, FW], w(2), t1[:, FW], ALU.mult, ALU.add,
                reads=[ar(A_PR + j), ar(T1)], writes=[ar(T1)])
            tt("dve", M(A_AB + j)[:, FW], M(A_AB + j)[:, FW], t1[:, FW], ALU.mult, reads=[ar(T1), ar(A_AB + j)],
               writes=[ar(A_AB + j)])
        group_norm([A_AB, A_AB + 1], 0, [T2])

        for j in range(2):
            for ti in range(ntile):
                m0, n = MT[ti]
                b = gbank()
                mms = [(psb[b][:, 0:n], dg[:, j * 31 + kk, :], vb[:, j, m0 - 30 + kk:m0 - 30 + kk + n], kk == 0, kk == 30)
                       for kk in range(31)]
                mm_group(mms, reads=[("dg", j * 31 + kk_) for kk_ in range(31)] + [ar(A_VB)], writes=[("ps", b)])
                act(M(A_Y + j)[:, m0:m0 + n], psb[b][:, 0:n], AF.Identity, bias=spc("bcc", l * 2 + j),
                    reads=[("ps", b), "spar"], writes=[ar(A_Y + j)])
        ycb = Mb(T2)
        ysq = Mb(A_T3)
        for j in range(2):
            act(ycb[:, j, FW], M(A_Y + j)[:, FW], AF.Copy, reads=[ar(A_Y + j)], writes=[ar(T2)])
            act(ysq[:, j, FW], M(A_Y + j)[:, FW], AF.Square, reads=[ar(A_Y + j)], writes=[ar(A_T3)])
        for ti in range(ntile):
            m0, n = MT[ti]
            b1, b2 = sbank(), sbank()
            mm_group([(psb[b1][:, 0:n], ones[:, :], ycb[:, j, m0:m0 + n], j == 0, j == 1) for j in range(2)],
                     reads=[ar(T2), "ones"], writes=[("ps", b1)])
            mm_group([(psb[b2][:, 0:n], ones[:, :], ysq[:, j, m0:m0 + n], j == 0, j == 1) for j in range(2)],
                     reads=[ar(A_T3), "ones"], writes=[("ps", b2)])
            mu = M(A_R)[:, m0:m0 + n]
            var = M(A_Q)[:, m0:m0 + n]
            ts("dve", mu, psb[b1][:, 0:n], 1.0 / 256, None, ALU.mult, reads=[("ps", b1)], writes=[ar(A_R)])
            tt("dve", var, mu, mu, ALU.mult, reads=[ar(A_R)], writes=[ar(A_Q)])
            stt("dve", var, psb[b2][:, 0:n], 1.0 / 256, var, ALU.mult, ALU.subtract, reads=[("ps", b2), ar(A_Q)],
                writes=[ar(A_Q)])
            ts("dve", var, var, 0.0, None, ALU.max, reads=[ar(A_Q)], writes=[ar(A_Q)])
            act(var, var, AF.Sqrt, bias=EPSC, scale=1.0, reads=[ar(A_Q), "cst"], writes=[ar(A_Q)])
            recip(var, var, reads=[ar(A_Q)], writes=[ar(A_Q)])
            for j in range(2):
                yj = M(A_Y + j)[:, m0:m0 + n]
                tt("dve", yj, yj, mu, ALU.subtract, reads=[ar(A_R), ar(A_Y + j)], writes=[ar(A_Y + j)])
                tt("dve", yj, yj, var, ALU.mult, reads=[ar(A_Q), ar(A_Y + j)], writes=[ar(A_Y + j)])
                act(yj, yj, AF.Silu, bias=spc("lnb", l * 2 + j), scale=spc("lng", l * 2 + j),
                    reads=[ar(A_Y + j), "spar"], writes=[ar(A_Y + j)])
        group_norm([A_Y, A_Y + 1], 6, [A_T3])

        A_PC = 0
        def b_vars(j):
            par = j % 2
            return par, [15, 8][par], [17, 9][par], [18, 14][par]

        def b_stage1(j):
            bx = M(A_BX + j)
            par, T1, A_R, A_Q = b_vars(j)
            xb = M(T1)
            w = lambda kk: spc("wcb", (l * 4 + j) * 4 + kk)
            ts("dve", xb[:, FW], bx[:, P0 - 3:WE - 3], w(0), spc("bcb", l * 4 + j), ALU.mult, ALU.add,
               reads=[ar(A_BX + j), "spar"], writes=[ar(T1)])
            for kk in range(1, 4):
                stt("dve", xb[:, FW], bx[:, P0 - 3 + kk:WE - 3 + kk], w(kk), xb[:, FW], ALU.mult, ALU.add,
                    reads=[ar(A_BX + j), ar(T1)], writes=[ar(T1)])
            xbb = Mb(T2)[:, par:par + 1, :]
            act(xbb[:, 0, FW], xb[:, FW], AF.Copy, reads=[ar(T1)], writes=[ar(T2)])
            rr = M(A_R)
            qq = M(A_Q)
            ii = M(A_BX + j)
            gwv = gw[:, :].rearrange("p (t c j) -> p t c j", t=2, c=4)
            for ti in range(ntile):
                m0, n = MT[ti]
                br, bi_ = gbank(), gbank()
                mm_group([(psb[br][:, 0:n], gwv[:, 0, j, :], xbb[:, 0, m0:m0 + n], True, True)],
                         reads=["gw", ar(T2)], writes=[("ps", br)])
                mm_group([(psb[bi_][:, 0:n], gwv[:, 1, j, :], xbb[:, 0, m0:m0 + n], True, True)],
                         reads=["gw", ar(T2)], writes=[("ps", bi_)])
                act(rr[:, m0:m0 + n], psb[br][:, 0:n], AF.Sigmoid, bias=spc("bgr", l * 4 + j),
                    reads=[("ps", br), "spar"], writes=[ar(A_R)])
                act(ii[:, m0:m0 + n], psb[bi_][:, 0:n], AF.Sigmoid, bias=spc("bgi", l * 4 + j),
                    reads=[("ps", bi_), "spar", ar(T1)], writes=[ar(A_BX + j)])
            act(qq[:, FW], rr[:, FW], AF.Exp, scale=s8[:, 2, l * 4 + j:l * 4 + j + 1], reads=[ar(A_R), "s8"],
                writes=[ar(A_Q)])
            act(qq[:, FW], qq[:, FW], AF.Sqrt, bias=ONE, scale=-1.0, reads=[ar(A_Q), "cst"], writes=[ar(A_Q)])
            act(rr[:, FW], rr[:, FW], AF.Exp, scale=s8[:, 1, l * 4 + j:l * 4 + j + 1], reads=[ar(A_R), "s8"],
                writes=[ar(A_R)])

        def b_stage2(j):
            par, T1, A_R, A_Q = b_vars(j)
            xb = M(T1)
            rr = M(A_R)
            qq = M(A_Q)
            ii = M(A_BX + j)
            tt("dve", qq[:, FW], qq[:, FW], ii[:, FW], ALU.mult, reads=[ar(A_Q), ar(A_BX + j)], writes=[ar(A_Q)])
            tt("dve", qq[:, FW], qq[:, FW], xb[:, FW], ALU.mult, reads=[ar(A_Q), ar(T1)], writes=[ar(A_Q)])
            if ph == 0:
                mset("dve", rr[:, P1:S0], 0.0, reads=[ar(A_R)], writes=[ar(A_R)])
                mset("dve", qq[:, P1:S0], 0.0, reads=[ar(A_Q)], writes=[ar(A_Q)])
                cp("dve", qq[:, S0 - 1:S0], states[:, l * 80 + 76 + j:l * 80 + 77 + j], reads=["states", ar(A_Q)],
                   writes=[ar(A_Q)])
            pr.op("dve", lambda e, o=ii[:, FW], a=rr[:, FW], b=qq[:, FW]: e.tensor_tensor_scan(
                out=o, data0=a, data1=b, initial=0.0, op0=ALU.mult, op1=ALU.add),
                reads=[ar(A_R), ar(A_Q)], writes=[ar(A_BX + j)])
            pr.op("dve", lambda e, o=M(A_PC + j)[:, FW], a=rr[:, FW]: e.tensor_tensor_scan(
                out=o, data0=a, data1=a, initial=1.0, op0=ALU.mult, op1=ALU.min),
                reads=[ar(A_R)], writes=[ar(A_PC + j)])
            cp("dve", stg2[:, j:j + 1], M(A_PC + j)[:, P1 - 1:P1], reads=[ar(A_PC + j)], writes=["stg2"])
            cp("dve", stg2[:, 4 + j:5 + j], ii[:, P1 - 1:P1], reads=[ar(A_BX + j)], writes=["stg2"])

        b_stage1(0)
        b_stage1(1)
        b_stage2(0)
        b_stage1(2)
        b_stage2(1)
        b_stage1(3)
        b_stage2(2)
        b_stage2(3)
        dma("sp", ib2[pl].ap()[:, :], stg2[:, :], "ccio", reads=["stg2"], writes=[("ib2", 0)])
        pr.dma("pool", lambda e: e.collective_compute("AllGather", ALU.bypass,
                                                       replica_groups=[list(range(NCORES))],
                                                       ins=[ib2[pl].ap().opt()], outs=[ob2[pl].ap().opt()]),
               "ccx", reads=[("ib2", 0)], writes=[("ob2", 0)], inc=1)
        dma("sp", g2[:, :, :], ob2[pl].ap().rearrange("(r p) w -> p r w", p=128), "ccio",
            reads=[("ob2", 0)], writes=["g2"])
        ne = 8 if ph == 0 else 16
        o0 = 0 if ph == 0 else 8
        ltv = spv("lt")
        for j in range(4):
            if ph == 1:
                cp("dve", chA[:, j, 0:8], g2save[:, l, :, j], reads=["g2save"], writes=["chA"])
                cp("dve", chB[:, j, 0:8], g2save[:, l, :, 4 + j], reads=["g2save"], writes=["chB"])
            ts("dve", chA[:, j, o0:o0 + 8], g2[:, :, j], -1.0, None, ALU.add, reads=["g2"], writes=["chA"])
            tt("dve", chA[:, j, o0:o0 + 8], chA[:, j, o0:o0 + 8], ltv, ALU.mult, reads=["chA", "spar"], writes=["chA"])
            ts("dve", chA[:, j, o0:o0 + 8], chA[:, j, o0:o0 + 8], 1.0, None, ALU.add, reads=["chA"], writes=["chA"])
            tt("dve", chB[:, j, o0:o0 + 8], g2[:, :, 4 + j], ltv, ALU.mult, reads=["g2", "spar"], writes=["chB"])
            pr.op("dve", lambda e, o=chO[:, j, 0:ne], a=chA[:, j, 0:ne], b=chB[:, j, 0:ne]: e.tensor_tensor_scan(
                out=o, data0=a, data1=b, initial=0.0, op0=ALU.mult, op1=ALU.add),
                reads=["chA", "chB"], writes=["chO"])
            cp("dve", hin[:, j:j + 1], chO[:, j, ne - 1:ne], reads=["chO"], writes=["hin"])
        if ph == 0:
            cp("dve", g2save[:, l, :, :], g2[:, :, :], reads=["g2"], writes=["g2save"])
        for j in range(4):
            hh = M(A_BX + j)
            stt("dve", hh[:, FW], M(A_PC + j)[:, FW], hin[:, j:j + 1], hh[:, FW], ALU.mult, ALU.add,
                reads=[ar(A_PC + j), "hin", ar(A_BX + j)], writes=[ar(A_BX + j)])
            cp("dve", stg1[:, 0, 76 + j:77 + j], hh[:, P1 - 1:P1], reads=[ar(A_BX + j)], writes=["stg1"])
            if ph == 0:
                cp("dve", stg1[:, 1, 76 + j:77 + j], hh[:, S1 - 1:S1], reads=[ar(A_BX + j)], writes=["stg1"])
            tt("dve", hh[:, FW], hh[:, FW], M(A_BG + j)[:, FW], ALU.mult, reads=[ar(A_BX + j), ar(A_BG + j)],
               writes=[ar(A_BX + j)])
        group_norm([A_BX + j for j in range(4)], 2, [17, 18])
        if ph == 1:
            dma("sp", tails_d[0, l], stg1[:, 0, :], "outt", reads=["stg1"], writes=[("tails", 0, l)])
        else:
            dma("sp", tails_d[1, l], stg1[:, 1, :], "outt", reads=["stg1"], writes=[("tails", 1, l)])
        pr.barrier()
        for q in range(4):
            slot = next_wa()
            for sidx in range(2):
                m = 2 * q + sidx
                for ti in range(ntile):
                    c0, n = CT[ti]
                    b = gbank()
                    mms = [(psb[b][:, 0:n], wa[:, slot, sidx, kc, :], h[:, kc, c0:c0 + n], kc == 0, kc == 7)
                           for kc in range(8)]
                    mm_group(mms, reads=[("wa", slot)] + [("h", c, ti) for c in range(8)], writes=[("ps", b)])
                    act(f_v[:, m, c0:c0 + n], psb[b][:, 0:n], AF.Copy, reads=[("ps", b)], writes=[("f", m, ti)])
                    act(g_v[:, m, c0:c0 + n], f_v[:, m, c0:c0 + n], AF.Square,
                        reads=[("f", m, ti)], writes=[("g", m, ti)])
            issue_wa()
        for ti in range(ntile):
            c0, n = CT[ti]
            bi = 1 if ti == 2 else 0
            r_ap, r_res = rstd_from([g_v[:, c, c0:c0 + n] for c in range(8)], D, n, [("g", c, ti) for c in range(8)])
            for c in range(8):
                k = nxt("a")
                stt("dve", tmpa[:, k, 0:n], f_v[:, c, c0:c0 + n], mcol(modG, bi, l, s, c), r_ap, ALU.mult, ALU.mult,
                    reads=[("f", c, ti), r_res, "modG"], writes=[("tmpa", k)])
                tt("dve", x[:, c, c0:c0 + n], x[:, c, c0:c0 + n], tmpa[:, k, 0:n], ALU.add,
                   reads=[("tmpa", k), ("x", c, ti)], writes=[("x", c, ti)])
        pr.barrier()

    stage = {"n": 0}

    def go():
        stage["n"] += 1
        return stage["n"] <= KSTOP
    for ph in range(2):
        if ph == 1:
            dma("sp", x[:, :, 0:SEG], xin[1], "ldx", writes=[("x", c, t) for c in range(8) for t in range(2)])
        for l in range(L):
            if go():
                ffn(l, 0, ph)
                pr.barrier()
            if go():
                mixer(l, ph)
            if go():
                ffn(l, 1, ph)
        dma("sp", yt_d[ph], x[:, :, 0:SEG], "outy", reads=[("x", c, t) for c in range(8) for t in range(2)],
            writes=[("yt", ph)])
        if ph == 0:
            dma("sp", ys_d, x[:, :, SEG:NT], "outys", reads=[("x", c, 2) for c in range(8)], writes=["ys"])
    pr.barrier()

    sem_names = list(pr.ENG) + sorted(pr.dcnt.keys())
    sems = {n: es.enter_context(nc.semaphore(f"s_{n}")) for n in sem_names}
    with nc.Block() as block:
        def replay(name, e):
            for waits, fn, ev, inc in pr.streams[name]:
                for k, v in waits:
                    e.wait_ge(sems[k], v)
                if fn is None:
                    continue
                inst = fn(e)
                inst.then_inc(sems[ev[0]], inc)

        @block.tensor
        def _(e):
            replay("pe", e)

        @block.scalar
        def _(e):
            replay("act", e)

        @block.vector
        def _(e):
            replay("dve", e)

        @block.gpsimd
        def _(e):
            replay("pool", e)

        @block.sync
        def _(e):
            replay("sp", e)
    es.close()
    print("[kernel] op counts", pr.cnt, {k: v for k, v in pr.dcnt.items()}, flush=True)
    return nc


def _fm(v, nch):
    sh = v.shape[:-1]
    a = v.reshape(sh + (nch, 128))
    a = np.moveaxis(a, -1, 0)
    return np.ascontiguousarray(a)


_NC_CACHE = {}


def kernel(x_prompt, x_sample, state_conv_a, state_conv_b, state_rglru, state_conv_c, c_prompt, c_sample,
           w_ada, b_ada, norm_pre, norm_post, ffn_w1, ffn_w3, ffn_w2, w_in, w_out, w_conv_a, w_conv_b, b_conv_b,
           w_gate_r, b_gate_r, w_gate_i, b_gate_i, rg_lambda, w_conv_c, b_conv_c, ln_c_g, ln_c_b, grp_g):
    f32 = np.float32
    A = lambda a: np.asarray(a, dtype=f32)
    x_prompt, x_sample = A(x_prompt), A(x_sample)
    ws = np.empty((L, NPIECE, 128, 2048), f32)
    w1, w3, w2 = A(ffn_w1), A(ffn_w3), A(ffn_w2)
    for l in range(L):
        for e in range(2):
            base = 0 if e == 0 else 35
            a1 = w1[l, e].reshape(8, 128, NF, 128).transpose(2, 1, 0, 3)
            a3 = w3[l, e].reshape(8, 128, NF, 128).transpose(2, 1, 0, 3)
            ws[l, base:base + NF] = np.stack([a1, a3], axis=2).reshape(NF, 128, 2048)
        wi = A(w_in)[l].reshape(8, 128, 9, 2, 128).transpose(2, 1, 3, 0, 4)
        ws[l, 22:31] = wi.reshape(9, 128, 2048)
        wo = A(w_out)[l].reshape(8, 128, 4, 2, 128).transpose(2, 1, 3, 0, 4)
        ws[l, 31:35] = wo.reshape(4, 128, 2048)
    wb = np.ascontiguousarray(w2.reshape(L, 2, NF, 128, 8, 128).transpose(0, 1, 4, 3, 2, 5)).reshape(L, 2, 8, 128, 2, 1408)
    wada_flat = A(w_ada).transpose(1, 0, 2).reshape(D, L * 9216)
    wada_all = wada_flat.reshape(8, 128, NCORES, NFC, 128)
    bada_all = A(b_ada).reshape(NCORES, NFC, 128)
    spar = np.zeros((128, NSP), f32)

    def put(name, arr):
        o, n = _SP[name]
        spar[:, o:o + n] = arr.reshape(128, n)
    put("npre", _fm(A(norm_pre), 8))
    put("npost", _fm(A(norm_post), 8))
    put("wca", _fm(A(w_conv_a), 2).transpose(0, 1, 3, 2))
    put("wcb", _fm(A(w_conv_b), 4).transpose(0, 1, 3, 2))
    put("bcb", _fm(A(b_conv_b), 4))
    put("bgr", _fm(A(b_gate_r), 4))
    put("bgi", _fm(A(b_gate_i), 4))
    put("lam", _fm(A(rg_lambda), 4))
    put("wcc", _fm(A(w_conv_c), 2).transpose(0, 1, 3, 2))
    put("bcc", _fm(A(b_conv_c), 2))
    put("lng", _fm(A(ln_c_g), 2))
    put("lnb", _fm(A(ln_c_b), 2))
    put("grp", _fm(A(grp_g), 8))
    gwh = np.zeros((128, L, 2, 4, 128), f32)
    for t, wg in enumerate((A(w_gate_r), A(w_gate_i))):
        for hh in range(8):
            c, o = hh // 2, (hh % 2) * 64
            gwh[o:o + 64, :, t, c, o:o + 64] = wg[:, hh].transpose(1, 0, 2)
    gwh = np.ascontiguousarray(gwh.transpose(1, 0, 2, 3, 4)).reshape(L, 128, 2 * 4 * 128)
    ident = np.eye(128, dtype=f32)
    cT = np.concatenate([A(c_prompt), A(c_sample)], axis=0)
    ct = np.ascontiguousarray(cT.reshape(9, 8, 128).transpose(2, 1, 0)).reshape(128, 72)
    sa, sb_, sh_, sc = A(state_conv_a), A(state_conv_b), A(state_rglru), A(state_conv_c)

    in_maps = []
    for k in range(NCORES):
        xin = np.empty((2, 128, 8, SEG), f32)
        for ph in range(2):
            seg = ph * 8 + k
            xin[ph] = x_prompt[0, seg * SEG:(seg + 1) * SEG].reshape(SEG, 8, 128).transpose(2, 1, 0)
        xs = np.ascontiguousarray(x_sample[k].reshape(NS, 8, 128).transpose(2, 1, 0))
        sp_k = spar.copy()
        o, _ = _SP["pm"]
        if k > 0:
            sp_k[:, o + k - 1] = 1.0
        else:
            sp_k[:, o + 8] = 1.0
        o, _ = _SP["lt"]
        sp_k[:, o:o + k] = 1.0
        o, _ = _SP["smask"]
        sp_k[:, o + k] = 1.0
        st = np.zeros((128, L, 80), f32)
        st[:, :, 0:4] = sa[:, k].reshape(L, 2, 2, 128).transpose(3, 0, 2, 1).reshape(128, L, 4)
        st[:, :, 4:16] = sb_[:, k].reshape(L, 3, 4, 128).transpose(3, 0, 2, 1).reshape(128, L, 12)
        st[:, :, 16:76] = sc[:, k].reshape(L, 30, 2, 128).transpose(3, 0, 2, 1).reshape(128, L, 60)
        st[:, :, 76:80] = sh_[:, k].reshape(L, 4, 128).transpose(2, 0, 1)
        wada_k = np.ascontiguousarray(wada_all[:, :, k].transpose(1, 2, 0, 3))
        wada_k = np.ascontiguousarray(wada_k.reshape(128, NFC // 3, 3 * 8 * 128).transpose(1, 0, 2))
        bada_k = np.repeat(bada_all[k].T[:, :, None], 9, axis=2).reshape(128, NMW)
        in_maps.append({
            "xin": xin, "xs": xs, "ws": ws, "wb": wb, "wada": wada_k, "bada": np.ascontiguousarray(bada_k),
            "ct": ct, "spar": sp_k, "gw": gwh, "ident": ident, "states": st.reshape(128, L * 80),
        })
        if KDUMMY:
            in_maps[-1]["dummyin"] = np.zeros((KDUMMY * 1024, 256), f32)

    if "nc" not in _NC_CACHE:
        _NC_CACHE["nc"] = build_program()
    nc = _NC_CACHE["nc"]
    res = run_bass_kernel_spmd(nc, in_maps, core_ids=list(range(NCORES)))
    R = res.results

    y_prompt = np.empty((1, 16 * SEG, D), f32)
    y_sample = np.empty((NCORES, NS, D), f32)
    for k in range(NCORES):
        yt = R[k]["yt"]
        for ph in range(2):
            seg = ph * 8 + k
            y_prompt[0, seg * SEG:(seg + 1) * SEG] = yt[ph].transpose(2, 1, 0).reshape(SEG, D)
        y_sample[k] = R[k]["ys"].transpose(2, 1, 0).reshape(NS, D)

    def unpack(t):
        ca = t[:, :, 0:4].reshape(L, 128, 2, 2).transpose(0, 3, 2, 1).reshape(L, 2, 256)
        cb = t[:, :, 4:16].reshape(L, 128, 4, 3).transpose(0, 3, 2, 1).reshape(L, 3, 512)
        cc = t[:, :, 16:76].reshape(L, 128, 2, 30).transpose(0, 3, 2, 1).reshape(L, 30, 256)
        hh = t[:, :, 76:80].transpose(0, 2, 1).reshape(L, 512)
        return ca, cb, hh, cc
    pa, pb, ph_, pc = unpack(R[NCORES - 1]["tails"][0])
    outs_p = [a[:, None] for a in (pa, pb, ph_, pc)]
    ss = [unpack(R[k]["tails"][1]) for k in range(NCORES)]
    outs_s = [np.stack([ss[k][i] for k in range(NCORES)], axis=1) for i in range(4)]
    return (y_prompt, y_sample,
            np.ascontiguousarray(outs_p[0]), np.ascontiguousarray(outs_p[1]),
            np.ascontiguousarray(outs_p[2]), np.ascontiguousarray(outs_p[3]),
            np.ascontiguousarray(outs_s[0]), np.ascontiguousarray(outs_s[1]),
            np.ascontiguousarray(outs_s[2]), np.ascontiguousarray(outs_s[3]))
```
